# Optimizing a Trainium2 kernel written in Bass

```python
import math
import jax, jax.numpy as jnp
from jax import lax
import numpy as np

D_MODEL = 2048
BATCH = 4
SEQ = 2048
DEPTH = 2

N_A = DEPTH // 2
N_B = DEPTH - N_A

M_HEADS = 8
M_QK_DIM = D_MODEL // 2 // M_HEADS
M_V_DIM = D_MODEL // M_HEADS
M_CHUNK = 64
GATE_SOFTCAP = 15.0
M_HQK = M_HEADS * M_QK_DIM
M_HV = M_HEADS * M_V_DIM
M_SPLITS = (M_HQK, 2 * M_HQK, 2 * M_HQK + M_HV, 2 * M_HQK + 2 * M_HV, 2 * M_HQK + 2 * M_HV + M_HEADS)
M_IN_COLS = 2 * M_HQK + 2 * M_HV + 2 * M_HEADS

A_HEADS = 8
A_QK_DIM = D_MODEL // (2 * A_HEADS)
A_V_DIM = 2 * A_QK_DIM
A_HQK = A_HEADS * 2 * A_QK_DIM
A_HV = A_HEADS * A_V_DIM
ROPE_DIM = A_QK_DIM // 4
ROPE_THETA = 500000.0
Q_BLOCK = 128

D_FF = ((8 * D_MODEL // 3 + 255) // 256) * 256
CONV_W = 3
EPS = 1e-6

kernel_name = "yoco_mlstm_diffattn_convffn"


def rms_norm(x, g):
    xf = x.astype(jnp.float32)
    y = xf * lax.rsqrt(jnp.mean(xf * xf, axis=-1, keepdims=True) + EPS)
    return (y * g.astype(jnp.float32)).astype(x.dtype)


def softcap(t):
    return GATE_SOFTCAP * jnp.tanh(t / GATE_SOFTCAP)


def mlstm_chunkwise(q, k, v, i_pre, log_f):
    B, H, S, Dk = q.shape
    Dv = v.shape[-1]
    L = M_CHUNK
    nc = S // L
    q = q * (Dk ** -0.5)

    def to_chunks(t):
        return jnp.moveaxis(t.reshape((B, H, nc, L) + t.shape[3:]), 2, 0)

    qc, kc, vc, ic, fc = (to_chunks(t) for t in (q, k, v, i_pre, log_f))
    causal = jnp.tril(jnp.ones((L, L), dtype=bool))

    def step(carry, inp):
        C, n, m = carry
        qb, kb, vb, ib, fb = inp
        b = jnp.cumsum(fb, axis=-1)
        dmat = b[..., :, None] - b[..., None, :] + ib[..., None, :]
        dmat = jnp.where(causal, dmat, -jnp.inf)
        inter = b + m[..., None]
        m_t = jnp.maximum(inter, jnp.max(dmat, axis=-1))
        w_inter = jnp.exp(inter - m_t)
        s = jnp.einsum('bhtd,bhsd->bhts', qb, kb) * jnp.exp(dmat - m_t[..., None])
        num = w_inter[..., None] * jnp.einsum('bhvd,bhtd->bhtv', C, qb) + jnp.einsum('bhts,bhsv->bhtv', s, vb)
        den = w_inter * jnp.einsum('bhd,bhtd->bht', n, qb) + jnp.sum(s, axis=-1)
        h = num / jnp.maximum(jnp.abs(den), jnp.exp(-m_t))[..., None]
        b_last = b[..., -1]
        g = b_last[..., None] - b + ib
        m_new = jnp.maximum(b_last + m, jnp.max(g, axis=-1))
        decay = jnp.exp(b_last + m - m_new)
        wg = jnp.exp(g - m_new[..., None])
        C_new = decay[..., None, None] * C + jnp.einsum('bhs,bhsv,bhsd->bhvd', wg, vb, kb)
        n_new = decay[..., None] * n + jnp.einsum('bhs,bhsd->bhd', wg, kb)
        return (C_new, n_new, m_new), h

    init = (jnp.zeros((B, H, Dv, Dk), jnp.float32),
            jnp.zeros((B, H, Dk), jnp.float32),
            jnp.zeros((B, H), jnp.float32))
    _, hc = lax.scan(step, init, (qc, kc, vc, ic, fc))
    return jnp.moveaxis(hc, 0, 2).reshape(B, H, S, Dv)


def mlstm_mixer(xn, w_in, b_igate, b_fgate, w_hnorm, w_out):
    B, S, _ = xn.shape
    proj = xn @ w_in
    q, k, v, o, ig, fg = jnp.split(proj, M_SPLITS, axis=-1)

    def heads(t, d):
        return t.reshape(B, S, M_HEADS, d).transpose(0, 2, 1, 3).astype(jnp.float32)

    q, k, v = heads(q, M_QK_DIM), heads(k, M_QK_DIM), heads(v, M_V_DIM)
    i_pre = softcap((ig + b_igate).astype(jnp.float32)).transpose(0, 2, 1)
    log_f = jax.nn.log_sigmoid(softcap((fg + b_fgate).astype(jnp.float32))).transpose(0, 2, 1)
    h = mlstm_chunkwise(q, k, v, i_pre, log_f)
    h = rms_norm(h.transpose(0, 2, 1, 3), w_hnorm).astype(xn.dtype)
    h = h.reshape(B, S, M_HV) * jax.nn.sigmoid(o)
    return h @ w_out


def rope_partial(x, pos):
    half = ROPE_DIM // 2
    inv_freq = ROPE_THETA ** (-jnp.arange(half, dtype=jnp.float32) / half)
    ang = pos.astype(jnp.float32)[..., None] * inv_freq
    cos = jnp.cos(ang)[:, :, None, :]
    sin = jnp.sin(ang)[:, :, None, :]
    x1 = x[..., :half].astype(jnp.float32)
    x2 = x[..., half:ROPE_DIM].astype(jnp.float32)
    rot = jnp.concatenate([x1 * cos - x2 * sin, x2 * cos + x1 * sin], axis=-1).astype(x.dtype)
    return jnp.concatenate([rot, x[..., ROPE_DIM:]], axis=-1)


def shared_kv(h, g_kv, w_kv, pos):
    B, S, _ = h.shape
    kv = rms_norm(h, g_kv) @ w_kv
    k, v = jnp.split(kv, (A_HQK,), axis=-1)
    k = rope_partial(k.reshape(B, S, 2 * A_HEADS, A_QK_DIM), pos).reshape(B, S, A_HEADS, 2, A_QK_DIM)
    v = v.reshape(B, S, A_HEADS, A_V_DIM)
    return k, v


def diff_attention(xn, k, v, pos, w_q, lam_q1, lam_k1, lam_q2, lam_k2, g_subln, w_o, lambda_init):
    B, S, _ = xn.shape
    q = rope_partial((xn @ w_q).reshape(B, S, 2 * A_HEADS, A_QK_DIM), pos)
    q = q.reshape(B, S, A_HEADS, 2, A_QK_DIM) * (A_QK_DIM ** -0.5)
    lam = (jnp.exp(jnp.sum(lam_q1.astype(jnp.float32) * lam_k1.astype(jnp.float32)))
           - jnp.exp(jnp.sum(lam_q2.astype(jnp.float32) * lam_k2.astype(jnp.float32))) + lambda_init)
    outs = []
    for start in range(0, S, Q_BLOCK):
        end = start + Q_BLOCK
        qb, kb, vb = q[:, start:end], k[:, :end], v[:, :end]
        s = jnp.einsum('bqhcd,bkhcd->bhcqk', qb, kb).astype(jnp.float32)
        mask = (start + jnp.arange(Q_BLOCK))[:, None] >= jnp.arange(end)[None, :]
        p = jax.nn.softmax(jnp.where(mask, s, -jnp.inf), axis=-1)
        pd = p[:, :, 0] - lam * p[:, :, 1]
        outs.append(jnp.einsum('bhqk,bkhv->bqhv', pd.astype(vb.dtype), vb))
    o = jnp.concatenate(outs, axis=1)
    o = rms_norm(o, g_subln) * (1.0 - lambda_init)
    return o.reshape(B, S, A_HV) @ w_o


def conv_ffn(xn, w_up, conv_w, conv_b, w_down):
    S = xn.shape[1]
    u = xn @ w_up
    up = jnp.pad(u, ((0, 0), (CONV_W - 1, 0), (0, 0)))
    c = conv_b + up[:, 0:S] * conv_w[0]
    for j in range(1, CONV_W):
        c = c + up[:, j:j + S] * conv_w[j]
    gate, val = jnp.split(c, 2, axis=-1)
    return (jax.nn.silu(gate) * val) @ w_down


def setup_inputs(seed: int = 0) -> dict:
    key = jax.random.key(seed)
    ks = jax.random.split(key, 32)
    f32 = jnp.float32
    nrm = lambda k, shape, scale: jax.random.normal(k, shape, f32) * scale
    gain = lambda k, shape: 1.0 + 0.05 * jax.random.normal(k, shape, f32)
    x = jax.random.normal(ks[0], (BATCH, SEQ, D_MODEL), f32)
    offset = jax.random.randint(ks[1], (BATCH, 1), 0, 4096, dtype=jnp.int32)
    positions = offset + jnp.arange(SEQ, dtype=jnp.int32)[None, :]
    fbias = jnp.linspace(3.0, 6.0, M_HEADS, dtype=f32)[None, :] + 0.1 * jax.random.normal(ks[5], (N_A, M_HEADS), f32)
    return {
        "x": x,
        "positions": positions,
        "a_norm": gain(ks[2], (N_A, D_MODEL)),
        "m_w_in": nrm(ks[3], (N_A, D_MODEL, M_IN_COLS), D_MODEL ** -0.5),
        "m_b_igate": nrm(ks[4], (N_A, M_HEADS), 0.1),
        "m_b_fgate": fbias,
        "m_w_hnorm": gain(ks[6], (N_A, M_HEADS, M_V_DIM)),
        "m_w_out": nrm(ks[7], (N_A, M_HV, D_MODEL), M_HV ** -0.5),
        "kv_norm": gain(ks[8], (D_MODEL,)),
        "w_kv": nrm(ks[9], (D_MODEL, A_HQK + A_HV), D_MODEL ** -0.5),
        "b_norm": gain(ks[10], (N_B, D_MODEL)),
        "w_q": nrm(ks[11], (N_B, D_MODEL, A_HQK), D_MODEL ** -0.5),
        "lam_q1": nrm(ks[12], (N_B, A_QK_DIM), 0.1),
        "lam_k1": nrm(ks[13], (N_B, A_QK_DIM), 0.1),
        "lam_q2": nrm(ks[14], (N_B, A_QK_DIM), 0.1),
        "lam_k2": nrm(ks[15], (N_B, A_QK_DIM), 0.1),
        "subln": gain(ks[16], (N_B, A_V_DIM)),
        "w_o": nrm(ks[17], (N_B, A_HV, D_MODEL), A_HV ** -0.5),
        "f_norm": gain(ks[18], (DEPTH, D_MODEL)),
        "w_up": nrm(ks[19], (DEPTH, D_MODEL, 2 * D_FF), D_MODEL ** -0.5),
        "conv_w": nrm(ks[20], (DEPTH, CONV_W, 2 * D_FF), CONV_W ** -0.5),
        "conv_b": nrm(ks[21], (DEPTH, 2 * D_FF), 0.01),
        "w_down": nrm(ks[22], (DEPTH, D_FF, D_MODEL), D_FF ** -0.5),
        "final_norm": gain(ks[23], (D_MODEL,)),
    }


def reference(x, positions, a_norm, m_w_in, m_b_igate, m_b_fgate, m_w_hnorm, m_w_out,
              kv_norm, w_kv, b_norm, w_q, lam_q1, lam_k1, lam_q2, lam_k2, subln, w_o,
              f_norm, w_up, conv_w, conv_b, w_down, final_norm):
    h = x
    k_sh = None
    v_sh = None
    for layer in range(DEPTH):
        if layer < N_A:
            h = h + mlstm_mixer(rms_norm(h, a_norm[layer]), m_w_in[layer], m_b_igate[layer],
                                m_b_fgate[layer], m_w_hnorm[layer], m_w_out[layer])
        else:
            j = layer - N_A
            if j == 0:
                k_sh, v_sh = shared_kv(h, kv_norm, w_kv, positions)
            lambda_init = 0.8 - 0.6 * math.exp(-0.3 * layer)
            h = h + diff_attention(rms_norm(h, b_norm[j]), k_sh, v_sh, positions, w_q[j],
                                   lam_q1[j], lam_k1[j], lam_q2[j], lam_k2[j], subln[j], w_o[j],
                                   lambda_init)
        h = h + conv_ffn(rms_norm(h, f_norm[layer]), w_up[layer], conv_w[layer], conv_b[layer], w_down[layer])
    return rms_norm(h, final_norm)
```

```python
from contextlib import ExitStack
import numpy as np
import concourse.bass as bass
import concourse.mybir as mybir
from concourse.bass_utils import run_bass_kernel_spmd

F32 = mybir.dt.float32
BF16 = mybir.dt.bfloat16
I32 = mybir.dt.int32
ALU = mybir.AluOpType
AF = mybir.ActivationFunctionType
ENGS = ("pe", "act", "dve", "pool", "sp")

D = 2048
TOK = 1024
NT = 8
KC = 16
DFF = 5632
NG = 44
EPS = 1e-6
NCORES = 8


class Buf:
    __slots__ = ("name", "w", "r", "sem", "semval")

    def __init__(self, name):
        self.name = name
        self.w = None
        self.r = []
        self.sem = None
        self.semval = 0


class Op:
    __slots__ = ("eng", "emit", "deps", "dma", "tok", "signal", "waits")

    def __init__(self, eng, emit, deps, dma):
        self.eng = eng
        self.emit = emit
        self.deps = deps
        self.dma = dma
        self.tok = None
        self.signal = False
        self.waits = []


class T:
    def __init__(self, t, name):
        self.t = t
        self.b = Buf(name)


class Prog:
    def __init__(self, nc):
        self.nc = nc
        self.ops = []
        self.es = ExitStack()
        self.dma_bufs = []
        self.last = {e: None for e in ENGS}
        self.out_dmas = []
        self._fence_pos = 0

    def sbuf(self, name, shape, dtype):
        return self.es.enter_context(self.nc.sbuf_tensor("sb_" + name, list(shape), dtype))

    def psum(self, name, shape, dtype):
        return self.es.enter_context(self.nc.psum_tensor(name, list(shape), dtype))

    def tile(self, name, shape, dtype):
        return T(self.sbuf(name, shape, dtype), name)

    def add(self, eng, emit, reads=(), writes=(), dma=None, is_out=False):
        idx = len(self.ops)
        deps = set()
        for b in reads:
            if b.w is not None:
                deps.add(b.w)
        for b in writes:
            if b.w is not None:
                deps.add(b.w)
            lastc = {}
            for r in b.r:
                ro = self.ops[r]
                if ro.dma is not None:
                    deps.add(r)
                else:
                    lastc[ro.eng] = max(lastc.get(ro.eng, -1), r)
            deps.update(lastc.values())
        o = Op(eng, emit, deps, dma)
        self.ops.append(o)
        for b in reads:
            b.r.append(idx)
        for b in writes:
            b.w = idx
            b.r = []
        if dma is not None and dma.sem is None:
            dma.sem = True
            self.dma_bufs.append(dma)
        self.last[eng] = idx
        if is_out:
            self.out_dmas.append(idx)
        return idx

    def dma(self, q, out, in_, sb, reads=(), writes=(), is_out=False):
        return self.add(q, lambda e: e.dma_start(out=out, in_=in_), reads=reads,
                        writes=writes, dma=sb, is_out=is_out)

    def fence(self):
        lasts = [v for v in self.last.values() if v is not None]
        dmas = [i for i in range(self._fence_pos, len(self.ops)) if self.ops[i].dma is not None]
        deps = set(lasts) | set(dmas)
        for e in ENGS:
            self.ops.append(Op(e, None, set(deps), None))
            self.last[e] = len(self.ops) - 1
        self._fence_pos = len(self.ops)

    def build(self):
        nc = self.nc
        ops = self.ops
        for i, o in enumerate(ops):
            for d in sorted(o.deps):
                p = ops[d]
                if p.emit is None and p.dma is None:
                    continue
                if (p.dma is None and o.dma is None and p.eng == "pe" and o.eng == "pe"
                        and o.emit is not None):
                    continue
                p.signal = True
                o.waits.append(d)
        es = self.es
        esem = {e: es.enter_context(nc.semaphore("s_" + e)) for e in ENGS}
        for k, b in enumerate(self.dma_bufs):
            b.sem = es.enter_context(nc.semaphore("d%d" % k))
            b.semval = 0
        cnt = {e: 0 for e in ENGS}
        for o in ops:
            if o.dma is not None:
                o.dma.semval += 16
                o.tok = (o.dma.sem, o.dma.semval)
            elif o.signal:
                cnt[o.eng] += 1
                o.tok = (esem[o.eng], cnt[o.eng])
        self.counts = dict(cnt)
        final_waits = list(self.out_dmas)
        block = es.enter_context(nc.Block())

        def run(engname):
            def f(eng):
                seen = {}

                def wait(d):
                    sem, val = ops[d].tok
                    k = id(sem)
                    if seen.get(k, 0) < val:
                        eng.wait_ge(sem, val)
                        seen[k] = val
                for o in ops:
                    if o.eng != engname:
                        continue
                    for d in o.waits:
                        wait(d)
                    if o.emit is None:
                        continue
                    ins = o.emit(eng)
                    if o.dma is not None:
                        ins.then_inc(o.tok[0], 16)
                    elif o.signal:
                        ins.then_inc(o.tok[0], 1)
                if engname == "sp":
                    for d in final_waits:
                        wait(d)
            return f

        block.tensor(run("pe"))
        block.scalar(run("act"))
        block.vector(run("dve"))
        block.gpsimd(run("pool"))
        block.sync(run("sp"))
        es.close()


class Ctx:
    def __init__(self, P, nc, ident_dram):
        self.P = P
        self.nc = nc
        self.ps = []
        for i in range(8):
            t = P.psum("ps%d" % i, [128, 512], F32)
            self.ps.append(T(t, "ps%d" % i))
        self.ident = P.tile("identb", [128, 128], BF16)
        P.dma("pool", self.ident.t[:, :], ident_dram, self.ident.b, writes=[self.ident.b])
        self.identf = P.tile("identf", [128, 128], F32)
        P.dma("sp", self.identf.t[:, :], ident_dram, self.identf.b, writes=[self.identf.b])
        self.epsc = P.tile("epsc", [128, 1], F32)
        P.add("dve", lambda e: e.memset(self.epsc.t[:, :], EPS), writes=[self.epsc.b])
        self.onec = P.tile('onec', [128, 1], F32)
        P.add('dve', lambda e: e.memset(self.onec.t[:, :], 1.0), writes=[self.onec.b])
        self.stat_i = 0

    def stat(self):
        self.stat_i += 1
        return self.P.tile('stat%d' % self.stat_i, [128, 4], F32)

    def psbf(self, i):
        return self.ps[i].t.bitcast(BF16)


def norm_transpose(C, h, gbc, xn, junk, stat, dst_fn, dstb, psi, ncols=128):
    P = C.P
    ss, rt, rstd = stat.t[:, 0:1], stat.t[:, 1:2], stat.t[:, 2:3]
    P.add("act", lambda e: e.activation(out=junk.t[:, :], in_=h.t[:, :], func=AF.Square, accum_out=ss),
          reads=[h.b], writes=[junk.b, stat.b])
    P.add("act", lambda e: e.activation(out=rt, in_=ss, func=AF.Sqrt, scale=1.0 / D, bias=C.epsc.t[:, 0:1]),
          reads=[stat.b, C.epsc.b], writes=[stat.b])
    P.add("dve", lambda e: e.reciprocal(out=rstd, in_=rt), reads=[stat.b], writes=[stat.b])
    P.add("dve", lambda e: e.scalar_tensor_tensor(out=xn.t[:, :], in0=h.t[:, :], scalar=rstd, in1=gbc.t[:, :],
                                                   op0=ALU.mult, op1=ALU.mult),
          reads=[h.b, stat.b, gbc.b], writes=[xn.b])
    for q in range(4):
        pb = C.ps[psi[q % len(psi)]]
        pbf = pb.t.bitcast(BF16)
        for j in range(4):
            kc = q * 4 + j
            P.add("pe", (lambda kc=kc, j=j, pbf=pbf: lambda e: e.transpose(
                out=pbf[:, j * 128:j * 128 + ncols], in_=xn.t[0:ncols, kc * 128:(kc + 1) * 128],
                identity=C.ident.t[0:ncols, 0:ncols]))(),
                reads=[xn.b, C.ident.b], writes=[pb.b])
        dst = dst_fn(q * 4)
        src = pbf[:, 0:512].rearrange("p (k t) -> p k t", k=4)[:, :, 0:ncols]
        if q % 2 == 0:
            P.add("act", (lambda dst=dst, src=src: lambda e: e.activation(out=dst, in_=src, func=AF.Copy))(),
                  reads=[pb.b], writes=[dstb])
        else:
            P.add("dve", (lambda dst=dst, src=src: lambda e: e.tensor_copy(out=dst, in_=src))(),
                  reads=[pb.b], writes=[dstb])


def ffn_phase(C, A1, A2, h_in, halo_in, gbc_dram, w_up, cw_dram, cb_dram, w_down, h_out,
              final_gbc=None, h_in_b=None, h_out_b=None, tag="f"):
    P, nc = C.P, C.nc
    h_in_b = h_in_b or Buf("h_in")
    h_out_b = h_out_b or Buf("h_out")

    def a1(off, n, dt=BF16):
        ap = A1.t[:, off:off + n]
        return ap if dt == BF16 else ap.bitcast(dt)

    def a2(off, n, dt=BF16):
        ap = A2.t[:, off:off + n]
        return ap if dt == BF16 else ap.bitcast(dt)

    class V:
        def __init__(self, ap, name):
            self.t = ap
            self.b = Buf(name)

    cw = P.tile(tag + "cw", [128, 3 * 88], F32)
    cb = P.tile(tag + "cb", [128, 88], F32)
    P.dma("sp", cw.t[:, :], cw_dram, cw.b, writes=[cw.b])
    P.dma("sp", cb.t[:, :], cb_dram, cb.b, writes=[cb.b])
    stat = [P.tile(tag + "stat%d" % i, [128, 4], F32) for i in range(2)]

    hb = [V(a1(i * 4096, 4096, F32), "hb%d" % i) for i in range(2)]
    junk = V(a1(8192, 2048), "junk")
    xn = [V(a1(10240 + i * 2048, 2048), "xn%d" % i) for i in range(2)]
    gbc = V(a1(14336, 4096, F32), "gbc")
    xnT = V(a2(0, 16384).rearrange("p (k t) -> p k t", k=16), "xnT")
    xnTh = V(a2(16384, 2048).rearrange("p (k t) -> p k t", k=16), "xnTh")
    P.dma("sp", gbc.t, gbc_dram, gbc.b, writes=[gbc.b])
    for tt in range(NT + 1):
        h = hb[tt % 2]
        src = h_in[tt * 128:(tt + 1) * 128, :] if tt < NT else halo_in
        P.dma("sp", h.t, src, h.b, reads=[h_in_b], writes=[h.b])
        if tt < NT:
            dst_fn = (lambda tt: lambda kc0: xnT.t[:, kc0:kc0 + 4, tt * 128:(tt + 1) * 128])(tt)
            dstb = xnT.b
        else:
            dst_fn = lambda kc0: xnTh.t[:, kc0:kc0 + 4, :]
            dstb = xnTh.b
        norm_transpose(C, h, gbc, xn[tt % 2], junk, stat[tt % 2], dst_fn, dstb, [6, 7])
    P.fence()

    actT = V(a1(0, 45056).rearrange("p (g t) -> p g t", g=NG), "actT")
    o = 18432
    wup = []
    for i in range(3):
        wup.append(V(a2(o, 4096).rearrange("p (k c) -> p k c", k=16), "wup%d" % i))
        o += 4096
    ubuf = []
    for i in range(4):
        ubuf.append(V(a2(o, 2056, F32), "ubuf%d" % i))
        o += 2056
    cbuf = []
    for i in range(4):
        cbuf.append(V(a2(o, 2048, F32), "cbuf%d" % i))
        o += 2048
    sg = []
    for i in range(2):
        sg.append(V(a2(o, 1024), "sg%d" % i))
        o += 1024
    assert o <= 49184, o

    def load_wup(g):
        w = wup[g % 3]
        for half in range(2):
            c0 = half * DFF + g * 128
            P.dma("pool", w.t[:, :, half * 128:(half + 1) * 128],
                  w_up[:, c0:c0 + 128].rearrange("(k p) c -> p k c", p=128), w.b, writes=[w.b])

    load_wup(0)
    load_wup(1)
    actT_parts = [Buf("actT%d" % g) for g in range(NG)]
    for g in range(NG):
        if g + 2 < NG:
            load_wup(g + 2)
        w = wup[g % 3]
        cvals = []
        for half in range(2):
            pA, pB, pH = C.ps[half * 3], C.ps[half * 3 + 1], C.ps[half * 3 + 2]
            for kc in range(KC):
                lhsT = w.t[:, kc, half * 128:(half + 1) * 128]
                for (pb, rhs, rb) in ((pA, xnT.t[:, kc, 0:512], xnT.b), (pB, xnT.t[:, kc, 512:1024], xnT.b),
                                      (pH, xnTh.t[:, kc, 0:2], xnTh.b)):
                    n = rhs.shape[-1]
                    P.add("pe", (lambda pb=pb, lhsT=lhsT, rhs=rhs, n=n, kc=kc: lambda e: e.matmul(
                        pb.t[:, 0:n], lhsT=lhsT, rhs=rhs, start=(kc == 0), stop=(kc == KC - 1)))(),
                        reads=[w.b, rb], writes=[pb.b])
            u = ubuf[(2 * g + half) % 4]
            c = cbuf[(2 * g + half) % 4]
            P.add("act", (lambda u=u, pA=pA: lambda e: e.activation(out=u.t[:, 2:514], in_=pA.t[:, :], func=AF.Copy))(),
                  reads=[pA.b], writes=[u.b])
            P.add("act", (lambda u=u, pB=pB: lambda e: e.activation(out=u.t[:, 514:1026], in_=pB.t[:, :], func=AF.Copy))(),
                  reads=[pB.b], writes=[u.b])
            P.add("act", (lambda u=u, pH=pH: lambda e: e.activation(out=u.t[:, 0:2], in_=pH.t[:, 0:2], func=AF.Copy))(),
                  reads=[pH.b], writes=[u.b])
            ch = half * NG + g
            w0, w1, w2 = (cw.t[:, j * 88 + ch:j * 88 + ch + 1] for j in range(3))
            bb = cb.t[:, ch:ch + 1]
            P.add("dve", (lambda u=u, c=c, w2=w2, bb=bb: lambda e: e.tensor_scalar(
                c.t[:, :], u.t[:, 2:1026], w2, bb, ALU.mult, ALU.add))(),
                reads=[u.b, cw.b, cb.b], writes=[c.b])
            P.add("dve", (lambda u=u, c=c, w1=w1: lambda e: e.scalar_tensor_tensor(
                out=c.t[:, :], in0=u.t[:, 1:1025], scalar=w1, in1=c.t[:, :], op0=ALU.mult, op1=ALU.add))(),
                reads=[u.b, cw.b, c.b], writes=[c.b])
            P.add("dve", (lambda u=u, c=c, w0=w0: lambda e: e.scalar_tensor_tensor(
                out=c.t[:, :], in0=u.t[:, 0:1024], scalar=w0, in1=c.t[:, :], op0=ALU.mult, op1=ALU.add))(),
                reads=[u.b, cw.b, c.b], writes=[c.b])
            cvals.append(c)
        s = sg[g % 2]
        P.add("act", (lambda s=s, c=cvals[0]: lambda e: e.activation(out=s.t[:, :], in_=c.t[:, :], func=AF.Silu))(),
              reads=[cvals[0].b], writes=[s.b])
        P.add("dve", (lambda s=s, c=cvals[1], g=g: lambda e: e.tensor_tensor(
            out=actT.t[:, g, :], in0=s.t[:, :], in1=c.t[:, :], op=ALU.mult))(),
            reads=[s.b, cvals[1].b], writes=[actT_parts[g]])
    P.fence()

    o = 0
    wd = []
    for i in range(2):
        wd.append(V(a2(o, 11264).rearrange("p (g c) -> p g c", g=22), "wd%d" % i))
        o += 11264
    hres = V(a2(o, 16384, F32).rearrange("p (t c) -> p t c", t=4), "hres")
    o += 16384
    fg = None
    if final_gbc is not None:
        fg = V(a2(o, 4096, F32), "fg")
        o += 4096
        P.dma("sp", fg.t, final_gbc, fg.b, writes=[fg.b])
        fjunk = V(a2(o, 4096, F32), "fjunk")
        o += 4096
    assert o <= 49184, o
    hres_parts = [Buf("hres%d" % i) for i in range(4)]
    nload = 0
    for tg in range(2):
        for tt in range(4):
            r0 = (tg * 4 + tt) * 128
            P.dma("sp", hres.t[:, tt, :], h_in[r0:r0 + 128, :], hres_parts[tt], reads=[h_in_b],
                  writes=[hres_parts[tt]])
        for cbk in range(4):
            for gh in range(2):
                w = wd[nload % 2]
                nload += 1
                P.dma("pool", w.t, w_down[gh * 2816:(gh + 1) * 2816, cbk * 512:(cbk + 1) * 512].rearrange(
                    "(g p) c -> p g c", p=128), w.b, writes=[w.b])
                for gl in range(22):
                    g = gh * 22 + gl
                    for tt in range(4):
                        pb = C.ps[tt + 4 * (cbk % 2)]
                        tok0 = (tg * 4 + tt) * 128
                        P.add("pe", (lambda pb=pb, g=g, gl=gl, tok0=tok0, w=w: lambda e: e.matmul(
                            pb.t[:, :], lhsT=actT.t[:, g, tok0:tok0 + 128], rhs=w.t[:, gl, :],
                            start=(g == 0), stop=(g == NG - 1)))(),
                            reads=[actT_parts[g], w.b], writes=[pb.b])
            for tt in range(4):
                pb = C.ps[tt + 4 * (cbk % 2)]
                P.add("dve", (lambda pb=pb, tt=tt, cbk=cbk: lambda e: e.tensor_tensor(
                    out=hres.t[:, tt, cbk * 512:(cbk + 1) * 512], in0=pb.t[:, :],
                    in1=hres.t[:, tt, cbk * 512:(cbk + 1) * 512], op=ALU.add))(),
                    reads=[pb.b, hres_parts[tt]], writes=[hres_parts[tt]])
        for tt in range(4):
            r0 = (tg * 4 + tt) * 128
            if fg is not None:
                st = stat[tt % 2]
                ss, rt, rstd = st.t[:, 0:1], st.t[:, 1:2], st.t[:, 2:3]
                hv = hres.t[:, tt, :]
                P.add("act", (lambda hv=hv, ss=ss: lambda e: e.activation(out=fjunk.t, in_=hv, func=AF.Square, accum_out=ss))(),
                      reads=[hres_parts[tt]], writes=[fjunk.b, st.b])
                P.add("act", (lambda ss=ss, rt=rt: lambda e: e.activation(out=rt, in_=ss, func=AF.Sqrt, scale=1.0 / D,
                                                                        bias=C.epsc.t[:, 0:1]))(),
                      reads=[st.b, C.epsc.b], writes=[st.b])
                P.add("dve", (lambda rt=rt, rstd=rstd: lambda e: e.reciprocal(out=rstd, in_=rt))(), reads=[st.b], writes=[st.b])
                P.add("dve", (lambda hv=hv, rstd=rstd: lambda e: e.scalar_tensor_tensor(
                    out=hv, in0=hv, scalar=rstd, in1=fg.t, op0=ALU.mult, op1=ALU.mult))(),
                    reads=[hres_parts[tt], st.b, fg.b], writes=[hres_parts[tt]])
            P.dma("sp", h_out[r0:r0 + 128, :], hres.t[:, tt, :], hres_parts[tt], reads=[hres_parts[tt]],
                  writes=[h_out_b], is_out=True)
    P.fence()


class V:
    def __init__(self, ap, name):
        self.t = ap
        self.b = Buf(name)


class Arena:
    def __init__(self, P, name, nel):
        self.tile = P.tile(name, [128, nel], BF16)
        self.nel = nel
        self.o = 0

    def reset(self):
        self.o = 0

    def take(self, n, dt=BF16, name="v"):
        k = n if dt == BF16 else 2 * n
        k = (k + 1) // 2 * 2
        assert self.o + k <= self.nel, (name, self.o, k, self.nel)
        ap = self.tile.t[:, self.o:self.o + k]
        self.o += k
        if dt != BF16:
            ap = ap.bitcast(dt)
        return V(ap, name)


def load_norm_T(C, ar, h_src_fn, ntiles, gbc_dram, xnT_dst_fn, xnT_b, src_b, psi=(6, 7), tag="n"):
    P = C.P
    hb = [ar.take(2048, F32, tag + "hb%d" % i) for i in range(2)]
    junk = ar.take(2048, BF16, tag + "junk")
    xn = [ar.take(2048, BF16, tag + "xn%d" % i) for i in range(2)]
    gbc = ar.take(2048, F32, tag + "gbc")
    stat = [C.stat(), C.stat()]
    P.dma("sp", gbc.t, gbc_dram, gbc.b, writes=[gbc.b])
    for tt in range(ntiles):
        h = hb[tt % 2]
        P.dma("sp", h.t, h_src_fn(tt), h.b, reads=[src_b], writes=[h.b])
        norm_transpose(C, h, gbc, xn[tt % 2], junk, stat[tt % 2], xnT_dst_fn(tt), xnT_b, list(psi))


def rope_tables(C, pos_dram, invf_dram, tag="r"):
    P = C.P
    posi = P.tile(tag + "posi", [128, 8], I32)
    posf = P.tile(tag + "posf", [128, 8], F32)
    invf = P.tile(tag + "invf", [128, 16], F32)
    ang = P.tile(tag + "ang", [128, 8 * 32], F32)
    tq = P.tile(tag + "tq", [128, 8 * 32], F32)
    ki = P.tile(tag + "ki", [128, 8 * 32], I32)
    kf = P.tile(tag + "kf", [128, 8 * 32], F32)
    sc = P.tile(tag + "sc", [128, 8 * 32], F32)
    cos4 = P.tile(tag + "cos4", [128, 8 * 64], F32)
    sin4 = P.tile(tag + "sin4", [128, 8 * 64], F32)
    P.dma("sp", posi.t[:, :], pos_dram, posi.b, writes=[posi.b])
    P.dma("sp", invf.t[:, :], invf_dram, invf.b, writes=[invf.b])
    P.add("dve", lambda e: e.tensor_copy(out=posf.t[:, :], in_=posi.t[:, :]), reads=[posi.b], writes=[posf.b])
    a3 = ang.t[:, :].rearrange("p (t s j) -> p t s j", t=8, s=2)
    for tt in range(8):
        P.add("dve", (lambda tt=tt: lambda e: e.tensor_scalar(a3[:, tt, 0, :], invf.t[:, :], posf.t[:, tt:tt + 1], None,
                                                              ALU.mult))(),
              reads=[posf.b, invf.b], writes=[ang.b])
        P.add("dve", (lambda tt=tt: lambda e: e.tensor_scalar(a3[:, tt, 1, :], invf.t[:, :], posf.t[:, tt:tt + 1],
                                                              float(np.pi / 2), ALU.mult, ALU.add))(),
              reads=[posf.b, invf.b], writes=[ang.b])
    TWO_PI = float(2 * np.pi)
    C1 = 6.28125
    C2 = TWO_PI - C1
    P.add("dve", lambda e: e.tensor_scalar(tq.t[:, :], ang.t[:, :], 1.0 / TWO_PI, None, ALU.mult),
          reads=[ang.b], writes=[tq.b])
    P.add("dve", lambda e: e.tensor_copy(out=ki.t[:, :], in_=tq.t[:, :]), reads=[tq.b], writes=[ki.b])
    P.add("dve", lambda e: e.tensor_copy(out=kf.t[:, :], in_=ki.t[:, :]), reads=[ki.b], writes=[kf.b])
    P.add("dve", lambda e: e.scalar_tensor_tensor(out=tq.t[:, :], in0=kf.t[:, :], scalar=-C1, in1=ang.t[:, :],
                                                   op0=ALU.mult, op1=ALU.add),
          reads=[kf.b, ang.b], writes=[tq.b])
    P.add("dve", lambda e: e.scalar_tensor_tensor(out=tq.t[:, :], in0=kf.t[:, :], scalar=-C2, in1=tq.t[:, :],
                                                   op0=ALU.mult, op1=ALU.add),
          reads=[kf.b, tq.b], writes=[tq.b])
    P.add("dve", lambda e: e.tensor_scalar(kf.t[:, :], tq.t[:, :], float(np.pi), -TWO_PI, ALU.is_gt, ALU.mult),
          reads=[tq.b], writes=[kf.b])
    P.add("dve", lambda e: e.tensor_tensor(out=tq.t[:, :], in0=tq.t[:, :], in1=kf.t[:, :], op=ALU.add),
          reads=[tq.b, kf.b], writes=[tq.b])
    P.add("dve", lambda e: e.tensor_scalar(kf.t[:, :], tq.t[:, :], float(-np.pi), TWO_PI, ALU.is_lt, ALU.mult),
          reads=[tq.b], writes=[kf.b])
    P.add("dve", lambda e: e.tensor_tensor(out=tq.t[:, :], in0=tq.t[:, :], in1=kf.t[:, :], op=ALU.add),
          reads=[tq.b, kf.b], writes=[tq.b])
    P.add("dve", lambda e: e.tensor_scalar(tq.t[:, :], tq.t[:, :], float(np.pi), float(-np.pi), ALU.min, ALU.max),
          reads=[tq.b], writes=[tq.b])
    P.add("act", lambda e: e.activation(out=sc.t[:, :], in_=tq.t[:, :], func=AF.Sin), reads=[tq.b], writes=[sc.b])
    s3 = sc.t[:, :].rearrange("p (t s j) -> p t s j", t=8, s=2)
    c4 = cos4.t[:, :].rearrange("p (t r j) -> p t r j", t=8, r=4)
    s4 = sin4.t[:, :].rearrange("p (t r j) -> p t r j", t=8, r=4)
    for r in range(4):
        P.add("dve", (lambda r=r: lambda e: e.tensor_copy(out=s4[:, :, r, :], in_=s3[:, :, 0, :]))(),
              reads=[sc.b], writes=[sin4.b])
        P.add("dve", (lambda r=r: lambda e: e.tensor_copy(out=c4[:, :, r, :], in_=s3[:, :, 1, :]))(),
              reads=[sc.b], writes=[cos4.b])
    return cos4, sin4


def rope_apply(C, x, cos4, sin4, tt, tmp):
    P = C.P
    x3 = x.t.rearrange("p (h d) -> p h d", h=4)
    x1, x2 = x3[:, :, 0:16], x3[:, :, 16:32]
    c = cos4.t[:, tt * 64:(tt + 1) * 64].rearrange("p (r j) -> p r j", r=4)
    s = sin4.t[:, tt * 64:(tt + 1) * 64].rearrange("p (r j) -> p r j", r=4)
    t4 = tmp.t.rearrange("p (k r j) -> p k r j", k=4, r=4)
    rd = [x.b, cos4.b, sin4.b]
    P.add("dve", lambda e: e.tensor_tensor(out=t4[:, 0], in0=x1, in1=c, op=ALU.mult), reads=rd, writes=[tmp.b])
    P.add("dve", lambda e: e.tensor_tensor(out=t4[:, 1], in0=x2, in1=s, op=ALU.mult), reads=rd, writes=[tmp.b])
    P.add("dve", lambda e: e.tensor_tensor(out=t4[:, 2], in0=x2, in1=c, op=ALU.mult), reads=rd, writes=[tmp.b])
    P.add("dve", lambda e: e.tensor_tensor(out=t4[:, 3], in0=x1, in1=s, op=ALU.mult), reads=rd, writes=[tmp.b])
    P.add("dve", lambda e: e.tensor_tensor(out=x1, in0=t4[:, 0], in1=t4[:, 1], op=ALU.subtract),
          reads=[tmp.b], writes=[x.b])
    P.add("dve", lambda e: e.tensor_tensor(out=x2, in0=t4[:, 2], in1=t4[:, 3], op=ALU.add),
          reads=[tmp.b], writes=[x.b])


def proj_block(C, xnT, w_dram_cols, wbuf, evac_fn, psi, ntt=NT):
    P = C.P
    P.dma("pool", wbuf.t, w_dram_cols.rearrange("(k p) c -> p k c", p=128), wbuf.b, writes=[wbuf.b])
    ncol = w_dram_cols.shape[-1]
    for tt in range(ntt):
        pb = C.ps[psi[tt % len(psi)]]
        for kc in range(KC):
            P.add("pe", (lambda pb=pb, kc=kc, tt=tt: lambda e: e.matmul(
                pb.t[:, 0:ncol], lhsT=xnT.t[:, kc, tt * 128:(tt + 1) * 128], rhs=wbuf.t[:, kc, 0:ncol],
                start=(kc == 0), stop=(kc == KC - 1)))(),
                reads=[xnT.b, wbuf.b], writes=[pb.b])
        evac_fn(tt, pb)


def kv_phase(C, ar, h_in, h_in_b, gbc_dram, w_kv, cos4, sin4, kT_out, v_out, kv_b, xnT=None):
    P = C.P
    ar.reset()
    xnT = ar.take(16 * 1024, BF16, "kv_xnT")
    x3 = xnT.t.rearrange("p (k t) -> p k t", k=16)
    xnT3 = V(x3, "kv_xnT3")
    xnT3.b = xnT.b
    load_norm_T(C, ar, lambda tt: h_in[tt * 128:(tt + 1) * 128, :], NT, gbc_dram,
                lambda tt: (lambda kc0: x3[:, kc0:kc0 + 4, tt * 128:(tt + 1) * 128]), xnT.b, h_in_b, tag="kvn")
    wb = []
    for i in range(2):
        w = ar.take(16 * 512, BF16, "kv_w%d" % i)
        w.t = w.t.rearrange("p (k c) -> p k c", k=16)
        wb.append(w)
    kx = [ar.take(512, F32, "kx%d" % i) for i in range(2)]
    kb = [ar.take(512, BF16, "kb%d" % i) for i in range(2)]
    tmp = ar.take(256, F32, "ropetmp")
    kTs = []
    for i in range(2):
        k_ = ar.take(4 * 1024, BF16, "kTs%d" % i)
        k_.t = k_.t.rearrange("p (h t) -> p h t", h=4)
        kTs.append(k_)
    vb = [ar.take(512, BF16, "vb%d" % i) for i in range(3)]
    cnt = [0]
    for cbk in range(8):
        w = wb[cbk % 2]
        if cbk < 4:
            ks = kTs[cbk % 2]

            def evac(tt, pb, ks=ks):
                i = cnt[0] % 2
                cnt[0] += 1
                x, b = kx[i], kb[i]
                P.add("act", lambda e: e.activation(out=x.t, in_=pb.t[:, :], func=AF.Copy), reads=[pb.b], writes=[x.b])
                rope_apply(C, x, cos4, sin4, tt, tmp)
                P.add("act", lambda e: e.activation(out=b.t, in_=x.t, func=AF.Copy), reads=[x.b], writes=[b.b])
                pt = C.ps[6 + (i % 2)]
                ptb = pt.t.bitcast(BF16)
                for j in range(4):
                    P.add("pe", (lambda j=j: lambda e: e.transpose(out=ptb[:, j * 128:(j + 1) * 128],
                                                                   in_=b.t[:, j * 128:(j + 1) * 128],
                                                                   identity=C.ident.t[:, :]))(),
                          reads=[b.b, C.ident.b], writes=[pt.b])
                P.add("dve", lambda e: e.tensor_copy(out=ks.t[:, :, tt * 128:(tt + 1) * 128],
                                                     in_=ptb[:, 0:512].rearrange("p (h t) -> p h t", h=4)),
                      reads=[pt.b], writes=[ks.b])
            proj_block(C, xnT3, w_kv[:, cbk * 512:(cbk + 1) * 512], w, evac, [0, 1, 2, 3])
            P.dma("sp", kT_out[cbk * 4:(cbk + 1) * 4].rearrange("h d t -> d h t"), ks.t, ks.b, reads=[ks.b],
                  writes=[kv_b], is_out=True)
        else:
            def evac(tt, pb, cbk=cbk):
                i = cnt[0] % 3
                cnt[0] += 1
                b = vb[i]
                P.add("act", lambda e: e.activation(out=b.t, in_=pb.t[:, :], func=AF.Copy), reads=[pb.b], writes=[b.b])
                P.dma("sp", v_out[tt * 128:(tt + 1) * 128, (cbk - 4) * 512:(cbk - 3) * 512], b.t, b.b,
                      reads=[b.b], writes=[kv_b], is_out=True)
            proj_block(C, xnT3, w_kv[:, cbk * 512:(cbk + 1) * 512], w, evac, [0, 1, 2, 3])
    P.fence()


def attn_phase(C, ar, h_in, h_in_b, gbc_dram, w_q, cos4, sin4, kT_prev, kT_own, v_prev, v_own, kv_b,
               lam_dram, subln_dram, tri_dram, w_o, h_out, h_out_b, lambda_init):
    P = C.P
    ar.reset()
    xnT = ar.take(16 * 1024, BF16, "at_xnT")
    x3 = xnT.t.rearrange("p (k t) -> p k t", k=16)
    xnT3 = V(x3, "x3")
    xnT3.b = xnT.b
    qT = ar.take(16 * 1024, BF16, "qT")
    q3 = qT.t.rearrange("p (h t) -> p h t", h=16)
    attn = ar.take(8 * 2048, BF16, "attn_tok")
    at3 = attn.t.rearrange("p (t c) -> p t c", t=8)
    attn_parts = [Buf("attn%d" % i) for i in range(8)]
    wb = []
    for i in range(2):
        w = ar.take(16 * 512, BF16, "at_w%d" % i)
        w.t = w.t.rearrange("p (k c) -> p k c", k=16)
        wb.append(w)
    mark = ar.o
    lamv = P.tile("lamv", [128, 512], F32)
    lamp = P.tile("lamp", [128, 512], F32)
    lams = P.tile("lams", [128, 4], F32)
    subg = P.tile("subg", [128, 256], F32)
    tri = P.tile("tri", [128, 128], BF16)
    P.dma("sp", lamv.t[:, :], lam_dram, lamv.b, writes=[lamv.b])
    P.dma("sp", subg.t[:, :], subln_dram, subg.b, writes=[subg.b])
    P.dma("pool", tri.t[:, :], tri_dram, tri.b, writes=[tri.b])
    for k in range(2):
        P.add("dve", (lambda k=k: lambda e: e.tensor_tensor(out=lamp.t[:, k * 128:(k + 1) * 128],
                                                            in0=lamv.t[:, k * 256:k * 256 + 128],
                                                            in1=lamv.t[:, k * 256 + 128:k * 256 + 256], op=ALU.mult))(),
              reads=[lamv.b], writes=[lamp.b])
        P.add("dve", (lambda k=k: lambda e: e.tensor_reduce(out=lams.t[:, k:k + 1], in_=lamp.t[:, k * 128:(k + 1) * 128],
                                                            axis=mybir.AxisListType.X, op=ALU.add))(),
              reads=[lamp.b], writes=[lams.b])
    P.add("act", lambda e: e.activation(out=lams.t[:, 0:2], in_=lams.t[:, 0:2], func=AF.Exp), reads=[lams.b], writes=[lams.b])
    P.add("dve", lambda e: e.tensor_tensor(out=lams.t[:, 2:3], in0=lams.t[:, 0:1], in1=lams.t[:, 1:2], op=ALU.subtract),
          reads=[lams.b], writes=[lams.b])
    P.add("dve", lambda e: e.tensor_scalar(lams.t[:, 3:4], lams.t[:, 2:3], float(lambda_init), -1.0, ALU.add, ALU.mult),
          reads=[lams.b], writes=[lams.b])
    neglam = lams.t[:, 3:4]
    P.add("dve", lambda e: e.tensor_scalar(subg.t[:, :], subg.t[:, :], float(1.0 - lambda_init), None, ALU.mult),
          reads=[subg.b], writes=[subg.b])

    load_norm_T(C, ar, lambda tt: h_in[tt * 128:(tt + 1) * 128, :], NT, gbc_dram,
                lambda tt: (lambda kc0: x3[:, kc0:kc0 + 4, tt * 128:(tt + 1) * 128]), xnT.b, h_in_b, tag="atn")
    kx = [ar.take(512, F32, "qx%d" % i) for i in range(2)]
    kb = [ar.take(512, BF16, "qb%d" % i) for i in range(2)]
    tmp = ar.take(256, F32, "qropetmp")
    cnt = [0]
    for cbk in range(4):
        def evac(tt, pb, cbk=cbk):
            i = cnt[0] % 2
            cnt[0] += 1
            x, b = kx[i], kb[i]
            P.add("act", lambda e: e.activation(out=x.t, in_=pb.t[:, :], func=AF.Copy), reads=[pb.b], writes=[x.b])
            rope_apply(C, x, cos4, sin4, tt, tmp)
            P.add("act", lambda e: e.activation(out=b.t, in_=x.t, func=AF.Copy), reads=[x.b], writes=[b.b])
            pt = C.ps[6 + (i % 2)]
            ptb = pt.t.bitcast(BF16)
            for j in range(4):
                P.add("pe", (lambda j=j: lambda e: e.transpose(out=ptb[:, j * 128:(j + 1) * 128],
                                                               in_=b.t[:, j * 128:(j + 1) * 128],
                                                               identity=C.ident.t[:, :]))(),
                      reads=[b.b, C.ident.b], writes=[pt.b])
            P.add("dve", lambda e: e.tensor_copy(out=q3[:, cbk * 4:(cbk + 1) * 4, tt * 128:(tt + 1) * 128],
                                                 in_=ptb[:, 0:512].rearrange("p (h t) -> p h t", h=4)),
                  reads=[pt.b], writes=[qT.b])
        proj_block(C, xnT3, w_q[:, cbk * 512:(cbk + 1) * 512], wb[cbk % 2], evac, [0, 1, 2, 3])
    P.fence()

    ar.o = mark
    kTh, vh = [], []
    for i in range(2):
        k_ = ar.take(2 * 2048, BF16, "kTh%d" % i)
        k_.t = k_.t.rearrange("p (c t) -> p c t", c=2)
        kTh.append(k_)
        v_ = ar.take(16 * 258, BF16, "vh%d" % i)
        v_.t = v_.t.rearrange("p (t c) -> p t c", t=16)
        vh.append(v_)
        P.add("dve", (lambda v_=v_: lambda e: e.memset(v_.t[:, :, 256:257], 1.0))(), writes=[v_.b])
        P.add("dve", (lambda v_=v_: lambda e: e.tensor_scalar(v_.t[:, 0:8, 256:257], v_.t[:, 0:8, 256:257],
                                                              C.flag.t[:, 0:1], None, ALU.mult))(),
              reads=[v_.b, C.flag.b], writes=[v_.b])
    pT = [ar.take(512, BF16, "pT%d" % i) for i in range(3)]
    o0 = [ar.take(256, F32, "o0_%d" % i) for i in range(8)]
    ot = [ar.take(256, F32, "ot%d" % i) for i in range(2)]
    rz = [C.stat() for i in range(4)]
    SC = float(128 ** -0.5)
    npt = [0]
    nev = [0]

    def load_head(h):
        k_, v_ = kTh[h % 2], vh[h % 2]
        P.dma("sp", k_.t[:, :, 0:1024], kT_prev[2 * h:2 * h + 2].rearrange("c d t -> d c t"), k_.b,
              reads=[kv_b], writes=[k_.b])
        P.dma("sp", k_.t[:, :, 1024:2048], kT_own[2 * h:2 * h + 2].rearrange("c d t -> d c t"), k_.b,
              reads=[kv_b], writes=[k_.b])
        P.dma("sp", v_.t[:, 0:8, 0:256], v_prev[:, h * 256:(h + 1) * 256].rearrange("(t p) c -> p t c", p=128),
              v_.b, reads=[kv_b], writes=[v_.b])
        P.dma("sp", v_.t[:, 8:16, 0:256], v_own[:, h * 256:(h + 1) * 256].rearrange("(t p) c -> p t c", p=128),
              v_.b, reads=[kv_b], writes=[v_.b])

    load_head(0)
    for h in range(8):
        if h + 1 < 8:
            load_head(h + 1)
        k_, v_ = kTh[h % 2], vh[h % 2]
        for c in range(2):
            hc = 2 * h + c
            for qb in range(2):
                acc = [C.ps[i] for i in range(4)]
                nkt = 8 + 4 * qb + 4
                for j in range(nkt):
                    jo = j - 8
                    i0 = max(jo - 4 * qb, 0) if jo >= 0 else 0
                    ncol = (4 - i0) * 128
                    ps = C.ps[4 + (npt[0] % 2)]
                    p_ = pT[npt[0] % 3]
                    npt[0] += 1
                    q0 = qb * 512 + i0 * 128
                    P.add("pe", (lambda ps=ps, j=j, q0=q0, ncol=ncol, c=c, hc=hc, k_=k_: lambda e: e.matmul(
                        ps.t[:, 0:ncol], lhsT=k_.t[:, c, j * 128:(j + 1) * 128], rhs=q3[:, hc, q0:q0 + ncol],
                        start=True, stop=True))(),
                        reads=[k_.b, qT.b], writes=[ps.b])
                    P.add("act", (lambda ps=ps, p_=p_, ncol=ncol: lambda e: e.activation(
                        out=p_.t[:, 0:ncol], in_=ps.t[:, 0:ncol], func=AF.Exp, scale=SC))(),
                        reads=[ps.b], writes=[p_.b])
                    if jo >= 0 and jo >= 4 * qb:
                        P.add("dve", (lambda p_=p_: lambda e: e.tensor_tensor(out=p_.t[:, 0:128], in0=p_.t[:, 0:128],
                                                                              in1=tri.t[:, :], op=ALU.mult))(),
                              reads=[p_.b, tri.b], writes=[p_.b])
                    for i in range(i0, 4):
                        last_j = 8 + 4 * qb + i
                        P.add("pe", (lambda i=i, i0=i0, j=j, p_=p_, last_j=last_j, v_=v_: lambda e: e.matmul(
                            acc[i].t[:, 0:257], lhsT=p_.t[:, (i - i0) * 128:(i - i0 + 1) * 128], rhs=v_.t[:, j, 0:257],
                            start=(j == 0), stop=(j == last_j)))(),
                            reads=[p_.b, v_.b], writes=[acc[i].b])
                for i in range(4):
                    tt = qb * 4 + i
                    r = rz[nev[0] % 4]
                    nev[0] += 1
                    P.add("dve", (lambda i=i, r=r: lambda e: e.reciprocal(out=r.t[:, 0:1], in_=acc[i].t[:, 256:257]))(),
                          reads=[acc[i].b], writes=[r.b])
                    if c == 0:
                        P.add("dve", (lambda i=i, r=r, tt=tt: lambda e: e.tensor_scalar(
                            o0[tt].t, acc[i].t[:, 0:256], r.t[:, 0:1], None, ALU.mult))(),
                            reads=[acc[i].b, r.b], writes=[o0[tt].b])
                    else:
                        o_ = ot[tt % 2]
                        P.add("dve", (lambda i=i, r=r, o_=o_: lambda e: e.tensor_scalar(
                            o_.t, acc[i].t[:, 0:256], r.t[:, 0:1], neglam, ALU.mult, ALU.mult))(),
                            reads=[acc[i].b, r.b, lams.b], writes=[o_.b])
                        P.add("dve", (lambda o_=o_, tt=tt: lambda e: e.tensor_tensor(out=o_.t, in0=o_.t, in1=o0[tt].t,
                                                                                      op=ALU.add))(),
                              reads=[o_.b, o0[tt].b], writes=[o_.b])
                        st = rz[nev[0] % 4]
                        nev[0] += 1
                        jk = o0[tt]
                        P.add("act", (lambda o_=o_, st=st, jk=jk: lambda e: e.activation(
                            out=jk.t, in_=o_.t, func=AF.Square, accum_out=st.t[:, 0:1]))(),
                            reads=[o_.b], writes=[jk.b, st.b])
                        P.add("act", (lambda st=st: lambda e: e.activation(out=st.t[:, 1:2], in_=st.t[:, 0:1], func=AF.Sqrt,
                                                                         scale=1.0 / 256, bias=C.epsc.t[:, 0:1]))(),
                              reads=[st.b, C.epsc.b], writes=[st.b])
                        P.add("dve", (lambda st=st: lambda e: e.reciprocal(out=st.t[:, 2:3], in_=st.t[:, 1:2]))(),
                              reads=[st.b], writes=[st.b])
                        P.add("dve", (lambda o_=o_, st=st, tt=tt, h=h: lambda e: e.scalar_tensor_tensor(
                            out=at3[:, tt, h * 256:(h + 1) * 256], in0=o_.t, scalar=st.t[:, 2:3], in1=subg.t[:, :],
                            op0=ALU.mult, op1=ALU.mult))(),
                            reads=[o_.b, st.b, subg.b], writes=[attn_parts[tt]])
    P.fence()

    ar.o = mark
    for tt in range(8):
        for q in range(4):
            pt = C.ps[6 + (q % 2)]
            ptb = pt.t.bitcast(BF16)
            for j in range(4):
                kc = q * 4 + j
                P.add("pe", (lambda kc=kc, j=j, tt=tt, ptb=ptb: lambda e: e.transpose(
                    out=ptb[:, j * 128:(j + 1) * 128], in_=at3[:, tt, kc * 128:(kc + 1) * 128],
                    identity=C.ident.t[:, :]))(),
                    reads=[attn_parts[tt], C.ident.b], writes=[pt.b])
            P.add("dve" if q % 2 else "act", (lambda q=q, tt=tt, ptb=ptb: (
                (lambda e: e.tensor_copy(out=x3[:, q * 4:(q + 1) * 4, tt * 128:(tt + 1) * 128],
                                         in_=ptb[:, 0:512].rearrange("p (k t) -> p k t", k=4))) if q % 2 else
                (lambda e: e.activation(out=x3[:, q * 4:(q + 1) * 4, tt * 128:(tt + 1) * 128],
                                        in_=ptb[:, 0:512].rearrange("p (k t) -> p k t", k=4), func=AF.Copy))))(),
                reads=[pt.b], writes=[xnT.b])
    residual_out_proj(C, ar, xnT3, w_o, wb, h_in, h_in_b, h_out, h_out_b)
    P.fence()


def residual_out_proj(C, ar, aT3, w, wb, h_in, h_in_b, h_out, h_out_b):
    P = C.P
    hr = [ar.take(512, F32, "hr%d" % i) for i in range(3)]
    cnt = [0]
    for cbk in range(4):
        def evac(tt, pb, cbk=cbk):
            r = hr[cnt[0] % 3]
            cnt[0] += 1
            P.dma("sp", r.t, h_in[tt * 128:(tt + 1) * 128, cbk * 512:(cbk + 1) * 512], r.b, reads=[h_in_b], writes=[r.b])
            P.add("dve", lambda e: e.tensor_tensor(out=r.t, in0=pb.t[:, :], in1=r.t, op=ALU.add),
                  reads=[pb.b, r.b], writes=[r.b])
            P.dma("sp", h_out[tt * 128:(tt + 1) * 128, cbk * 512:(cbk + 1) * 512], r.t, r.b, reads=[r.b],
                  writes=[h_out_b], is_out=True)
        proj_block(C, aT3, w[:, cbk * 512:(cbk + 1) * 512], wb[cbk % 2], evac, [0, 1, 2, 3])


def mlstm_phase(C, ar, x_own, x_prev, x_b, gbc_dram, w_in, bias_dram, whn_dram, tri_dram, negm_dram, ones_dram,
                w_out, h_out, h_out_b):
    P = C.P
    ar.reset()
    xo = ar.take(16 * 1024, BF16, "xnT_own")
    xo3 = V(xo.t.rearrange("p (k t) -> p k t", k=16), "xo3")
    xo3.b = xo.b
    xp = ar.take(16 * 1024, BF16, "xnT_prev")
    xp3 = V(xp.t.rearrange("p (k t) -> p k t", k=16), "xp3")
    xp3.b = xp.b
    hg = ar.take(8 * 2048, BF16, "hg_tok")
    hg3 = hg.t.rearrange("p (t c) -> p t c", t=8)
    hg_parts = [Buf("hg%d" % i) for i in range(8)]
    wb = []
    for i in range(2):
        w = ar.take(16 * 768, BF16, "min_w%d" % i)
        w.t = w.t.rearrange("p (k c) -> p k c", k=16)
        wb.append(w)
    mark = ar.o
    triS = P.tile("triS", [128, 128], F32)
    negm = P.tile("negm", [128, 128], F32)
    onesf = P.tile("onesf", [128, 128], F32)
    biasg = P.tile("biasg", [128, 16], F32)
    whn = P.tile("whn", [128, 2048], F32)
    P.dma("sp", triS.t[:, :], tri_dram, triS.b, writes=[triS.b])
    P.dma("sp", negm.t[:, :], negm_dram, negm.b, writes=[negm.b])
    P.dma("sp", onesf.t[:, :], ones_dram, onesf.b, writes=[onesf.b])
    P.dma("sp", biasg.t[:, :], bias_dram, biasg.b, writes=[biasg.b])
    P.dma("sp", whn.t[:, :], whn_dram, whn.b, writes=[whn.b])
    G = P.tile("gates", [128, 16 * 16], F32)
    G3 = G.t[:, :].rearrange("p (t c) -> p t c", t=16)
    BC = P.tile("bcum", [128, 16 * 8], F32)
    BC3 = BC.t[:, :].rearrange("p (t c) -> p t c", t=16)
    BT = P.tile("btot", [128, 16 * 8], F32)
    BT3 = BT.t[:, :].rearrange("p (t c) -> p t c", t=16)
    AS = P.tile("a_s", [128, 16 * 8], F32)
    WG = P.tile("wg", [128, 16 * 8], F32)
    EB = P.tile("eb", [128, 16 * 8], F32)
    DC = P.tile("decay", [128, 16 * 8], F32)
    AS3, WG3, EB3, DC3 = (t_.t[:, :].rearrange("p (t c) -> p t c", t=16) for t_ in (AS, WG, EB, DC))
    wgate = P.tile("wgate", [128, 16 * 16], BF16)
    wgate3 = wgate.t[:, :].rearrange("p (k c) -> p k c", k=16)
    P.dma("pool", wgate3, w_in[:, 6144:6160].rearrange("(k p) c -> p k c", p=128), wgate.b, writes=[wgate.b])

    load_norm_T(C, ar, lambda tt: x_own[tt * 128:(tt + 1) * 128, :], NT, gbc_dram,
                lambda tt: (lambda kc0: xo3.t[:, kc0:kc0 + 4, tt * 128:(tt + 1) * 128]), xo.b, x_b, tag="mo")
    P.fence()
    ar.o = mark
    load_norm_T(C, ar, lambda tt: x_prev[tt * 128:(tt + 1) * 128, :], NT, gbc_dram,
                lambda tt: (lambda kc0: xp3.t[:, kc0:kc0 + 4, tt * 128:(tt + 1) * 128]), xp.b, x_b, tag="mp")
    for t16 in range(16):
        src = xp3 if t16 < 8 else xo3
        tt = t16 % 8
        pb = C.ps[t16 % 2]
        for kc in range(KC):
            P.add("pe", (lambda pb=pb, kc=kc, tt=tt, src=src: lambda e: e.matmul(
                pb.t[:, 0:16], lhsT=src.t[:, kc, tt * 128:(tt + 1) * 128], rhs=wgate3[:, kc, :],
                start=(kc == 0), stop=(kc == KC - 1)))(),
                reads=[src.b, wgate.b], writes=[pb.b])
        P.add("dve", (lambda pb=pb, t16=t16: lambda e: e.tensor_tensor(out=G3[:, t16, :], in0=pb.t[:, 0:16],
                                                                        in1=biasg.t[:, :], op=ALU.add))(),
              reads=[pb.b, biasg.b], writes=[G.b])
    P.add("act", lambda e: e.activation(out=G.t[:, :], in_=G.t[:, :], func=AF.Tanh, scale=1.0 / 15.0), reads=[G.b], writes=[G.b])
    P.add("dve", lambda e: e.tensor_scalar(G.t[:, :], G.t[:, :], 15.0, None, ALU.mult), reads=[G.b], writes=[G.b])
    P.add("act", lambda e: e.activation(out=G3[:, :, 8:16], in_=G3[:, :, 8:16], func=AF.Exp, scale=-1.0), reads=[G.b], writes=[G.b])
    P.add("act", lambda e: e.activation(out=G3[:, :, 8:16], in_=G3[:, :, 8:16], func=AF.Ln, bias=C.onec.t[:, 0:1]),
          reads=[G.b, C.onec.b], writes=[G.b])
    P.add("dve", lambda e: e.tensor_scalar(G3[:, :, 8:16], G3[:, :, 8:16], -1.0, None, ALU.mult), reads=[G.b], writes=[G.b])
    for t16 in range(16):
        pb = C.ps[2 + t16 % 2]
        P.add("pe", (lambda pb=pb, t16=t16: lambda e: e.matmul(pb.t[:, 0:8], lhsT=triS.t[:, :], rhs=G3[:, t16, 8:16],
                                                               start=True, stop=True))(),
              reads=[triS.b, G.b], writes=[pb.b])
        P.add("pe", (lambda pb=pb, t16=t16: lambda e: e.matmul(pb.t[:, 8:16], lhsT=onesf.t[:, :], rhs=G3[:, t16, 8:16],
                                                               start=True, stop=True))(),
              reads=[onesf.b, G.b], writes=[pb.b])
        P.add("dve", (lambda pb=pb, t16=t16: lambda e: e.tensor_copy(out=BC3[:, t16, :], in_=pb.t[:, 0:8]))(),
              reads=[pb.b], writes=[BC.b])
        P.add("dve", (lambda pb=pb, t16=t16: lambda e: e.tensor_copy(out=BT3[:, t16, :], in_=pb.t[:, 8:16]))(),
              reads=[pb.b], writes=[BT.b])
    P.add("dve", lambda e: e.tensor_tensor(out=AS3, in0=G3[:, :, 0:8], in1=BC3, op=ALU.subtract), reads=[G.b, BC.b], writes=[AS.b])
    P.add("dve", lambda e: e.tensor_tensor(out=WG3, in0=AS3, in1=BT3, op=ALU.add), reads=[AS.b, BT.b], writes=[WG.b])
    P.add("act", lambda e: e.activation(out=WG.t[:, :], in_=WG.t[:, :], func=AF.Exp), reads=[WG.b], writes=[WG.b])
    P.add("act", lambda e: e.activation(out=EB.t[:, :], in_=BC.t[:, :], func=AF.Exp), reads=[BC.b], writes=[EB.b])
    P.add("act", lambda e: e.activation(out=DC.t[:, :], in_=BT.t[:, :], func=AF.Exp), reads=[BT.b], writes=[DC.b])
    P.fence()

    ar.o = mark
    qT = ar.take(1024, BF16, "m_qT")
    kT = ar.take(1024, BF16, "m_kT")
    kTp = ar.take(1024, BF16, "m_kTp")
    ktok = ar.take(8 * 128, BF16, "m_ktok")
    ktokp = ar.take(8 * 128, BF16, "m_ktokp")
    vo = ar.take(8 * 258, BF16, "m_v")
    vp = ar.take(8 * 258, BF16, "m_vp")
    so = ar.take(8 * 256, BF16, "m_so")
    kt3 = ktok.t.rearrange("p (t d) -> p t d", t=8)
    ktp3 = ktokp.t.rearrange("p (t d) -> p t d", t=8)
    vo3 = vo.t.rearrange("p (t c) -> p t c", t=8)
    vp3 = vp.t.rearrange("p (t c) -> p t c", t=8)
    so3 = so.t.rearrange("p (t c) -> p t c", t=8)
    P.add("dve", lambda e: e.memset(vo3[:, :, 256:257], 1.0), writes=[vo.b])
    P.add("dve", lambda e: e.memset(vp3[:, :, 256:257], 1.0), writes=[vp.b])
    Cst = ar.take(257, F32, "Cstate")
    Cb = ar.take(258, BF16, "Cstate_bf")
    kw = [ar.take(128, BF16, "kw%d" % i) for i in range(2)]
    rhs_e = [ar.take(128, F32, "rhs_e%d" % i) for i in range(2)]
    DT = [ar.take(128, F32, "DT%d" % i) for i in range(2)]
    PT = [ar.take(128, BF16, "PT%d" % i) for i in range(2)]
    n1 = [ar.take(257, F32, "n1_%d" % i) for i in range(2)]
    num = [ar.take(257, F32, "num%d" % i) for i in range(2)]
    hh = [ar.take(256, F32, "hh%d" % i) for i in range(2)]
    jk = ar.take(256, F32, "mjunk")
    st = [C.stat() for i in range(4)]
    QS = float(128 ** -0.5)

    def load_w(hd):
        w = wb[hd % 2]
        for (dst0, c0, n) in ((0, hd * 128, 128), (128, 1024 + hd * 128, 128), (256, 2048 + hd * 256, 256),
                              (512, 4096 + hd * 256, 256)):
            P.dma("pool", w.t[:, :, dst0:dst0 + n], w_in[:, c0:c0 + n].rearrange("(k p) c -> p k c", p=128), w.b,
                  writes=[w.b])

    def projB(w, c0, src3, dst, scale, pbi):
        for tb in range(2):
            pb = C.ps[pbi + tb]
            for kc in range(KC):
                P.add("pe", (lambda pb=pb, kc=kc, tb=tb: lambda e: e.matmul(
                    pb.t[:, :], lhsT=w.t[:, kc, c0:c0 + 128], rhs=src3.t[:, kc, tb * 512:(tb + 1) * 512],
                    start=(kc == 0), stop=(kc == KC - 1)))(),
                    reads=[w.b, src3.b], writes=[pb.b])
            P.add("act", (lambda pb=pb, tb=tb: lambda e: e.activation(out=dst.t[:, tb * 512:(tb + 1) * 512], in_=pb.t[:, :],
                                                                      func=AF.Copy, scale=scale))(),
                  reads=[pb.b], writes=[dst.b])

    def to_tok(srcT, dst3, dstb):
        for half in range(2):
            pt = C.ps[7]
            ptb = pt.t.bitcast(BF16)
            for j in range(4):
                tt = half * 4 + j
                P.add("pe", (lambda tt=tt, j=j, ptb=ptb: lambda e: e.transpose(
                    out=ptb[:, j * 128:(j + 1) * 128], in_=srcT.t[:, tt * 128:(tt + 1) * 128], identity=C.ident.t[:, :]))(),
                    reads=[srcT.b, C.ident.b], writes=[pt.b])
            P.add("dve", (lambda half=half, ptb=ptb: lambda e: e.tensor_copy(
                out=dst3[:, half * 4:(half + 1) * 4, :], in_=ptb[:, 0:512].rearrange("p (t d) -> p t d", t=4)))(),
                reads=[pt.b], writes=[dstb])

    def state_update(k3, v3, vb_, t16, hd, first, ki):
        tt = t16 % 8
        k_ = kw[ki % 2]
        P.add("dve", lambda e: e.tensor_scalar(k_.t, k3[:, tt, :], WG3[:, t16, hd:hd + 1], None, ALU.mult),
              reads=[WG.b, ktok.b, ktokp.b], writes=[k_.b])
        pc = C.ps[6]
        P.add("pe", lambda e: e.matmul(pc.t[:, 0:257], lhsT=k_.t, rhs=v3[:, tt, 0:257], start=True, stop=True),
              reads=[k_.b, vb_], writes=[pc.b])
        if first:
            P.add("dve", lambda e: e.tensor_copy(out=Cst.t, in_=pc.t[:, 0:257]), reads=[pc.b], writes=[Cst.b])
        else:
            P.add("dve", lambda e: e.scalar_tensor_tensor(out=Cst.t, in0=Cst.t, scalar=DC3[:, t16, hd:hd + 1],
                                                           in1=pc.t[:, 0:257], op0=ALU.mult, op1=ALU.add),
                  reads=[Cst.b, DC.b, pc.b], writes=[Cst.b])
        P.add("act", lambda e: e.activation(out=Cb.t[:, 0:257], in_=Cst.t, func=AF.Copy), reads=[Cst.b], writes=[Cb.b])

    load_w(0)
    it = [0]
    for hd in range(8):
        if hd + 1 < 8:
            load_w(hd + 1)
        w = wb[hd % 2]
        projB(w, 128, xp3, kTp, 1.0, 0)
        to_tok(kTp, ktp3, ktokp.b)
        for tt in range(8):
            pb = C.ps[tt % 2]
            for kc in range(KC):
                P.add("pe", (lambda pb=pb, kc=kc, tt=tt, w=w: lambda e: e.matmul(
                    pb.t[:, 0:256], lhsT=xp3.t[:, kc, tt * 128:(tt + 1) * 128], rhs=w.t[:, kc, 256:512],
                    start=(kc == 0), stop=(kc == KC - 1)))(),
                    reads=[w.b, xp.b], writes=[pb.b])
            P.add("act", (lambda pb=pb, tt=tt: lambda e: e.activation(out=vp3[:, tt, 0:256], in_=pb.t[:, 0:256], func=AF.Copy))(),
                  reads=[pb.b], writes=[vp.b])
        projB(w, 0, xo3, qT, QS, 0)
        projB(w, 128, xo3, kT, 1.0, 0)
        to_tok(kT, kt3, ktok.b)
        for tt in range(8):
            pb = C.ps[tt % 2]
            for kc in range(KC):
                P.add("pe", (lambda pb=pb, kc=kc, tt=tt, w=w: lambda e: e.matmul(
                    pb.t[:, :], lhsT=xo3.t[:, kc, tt * 128:(tt + 1) * 128], rhs=w.t[:, kc, 256:768],
                    start=(kc == 0), stop=(kc == KC - 1)))(),
                    reads=[w.b, xo.b], writes=[pb.b])
            P.add("act", (lambda pb=pb, tt=tt: lambda e: e.activation(out=vo3[:, tt, 0:256], in_=pb.t[:, 0:256], func=AF.Copy))(),
                  reads=[pb.b], writes=[vo.b])
            P.add("act", (lambda pb=pb, tt=tt: lambda e: e.activation(out=so3[:, tt, :], in_=pb.t[:, 256:512], func=AF.Sigmoid))(),
                  reads=[pb.b], writes=[so.b])
        for tt in range(8):
            state_update(ktp3, vp3, vp.b, tt, hd, tt == 0, it[0])
            it[0] += 1
        for tt in range(8):
            t16 = 8 + tt
            i = it[0]
            it[0] += 1
            re_, d_, p_, n1_, nm, h_ = rhs_e[i % 2], DT[i % 2], PT[i % 2], n1[i % 2], num[i % 2], hh[i % 2]
            s0, s1 = st[(2 * i) % 4], st[(2 * i + 1) % 4]
            pS, pE, pN1, pN2 = C.ps[2], C.ps[3], C.ps[4], C.ps[5]
            P.add("pe", (lambda tt=tt: lambda e: e.matmul(pS.t[:, 0:128], lhsT=kT.t[:, tt * 128:(tt + 1) * 128],
                                                          rhs=qT.t[:, tt * 128:(tt + 1) * 128], start=True, stop=True))(),
                  reads=[kT.b, qT.b], writes=[pS.b])
            P.add("dve", (lambda t16=t16, hd=hd, re_=re_: lambda e: e.tensor_scalar(
                re_.t, triS.t[:, :], G3[:, t16, 8 + hd:9 + hd], None, ALU.mult))(),
                reads=[triS.b, G.b], writes=[re_.b])
            P.add("pe", (lambda re_=re_: lambda e: e.matmul(pE.t[:, 0:128], lhsT=onesf.t[:, :], rhs=re_.t, start=True, stop=False))(),
                  reads=[onesf.b, re_.b], writes=[pE.b])
            P.add("pe", lambda e: e.matmul(pE.t[:, 0:128], lhsT=C.identf.t[:, :], rhs=negm.t[:, :], start=False, stop=True),
                  reads=[C.identf.b, negm.b], writes=[pE.b])
            P.add("act", (lambda t16=t16, hd=hd, d_=d_: lambda e: e.activation(
                out=d_.t, in_=pE.t[:, 0:128], func=AF.Exp, bias=AS3[:, t16, hd:hd + 1]))(),
                reads=[pE.b, AS.b], writes=[d_.b])
            P.add("dve", (lambda d_=d_, p_=p_: lambda e: e.tensor_tensor(out=p_.t, in0=pS.t[:, 0:128], in1=d_.t, op=ALU.mult))(),
                  reads=[pS.b, d_.b], writes=[p_.b])
            P.add("pe", (lambda tt=tt, p_=p_: lambda e: e.matmul(pN1.t[:, 0:257], lhsT=p_.t, rhs=vo3[:, tt, 0:257],
                                                                 start=True, stop=True))(),
                  reads=[p_.b, vo.b], writes=[pN1.b])
            P.add("pe", (lambda tt=tt: lambda e: e.matmul(pN2.t[:, 0:257], lhsT=qT.t[:, tt * 128:(tt + 1) * 128],
                                                          rhs=Cb.t[:, 0:257], start=True, stop=True))(),
                  reads=[qT.b, Cb.b], writes=[pN2.b])
            P.add("act", (lambda n1_=n1_: lambda e: e.activation(out=n1_.t, in_=pN1.t[:, 0:257], func=AF.Copy))(),
                  reads=[pN1.b], writes=[n1_.b])
            P.add("dve", (lambda n1_=n1_, nm=nm, t16=t16, hd=hd: lambda e: e.scalar_tensor_tensor(
                out=nm.t, in0=pN2.t[:, 0:257], scalar=EB3[:, t16, hd:hd + 1], in1=n1_.t, op0=ALU.mult, op1=ALU.add))(),
                reads=[pN2.b, EB.b, n1_.b], writes=[nm.b])
            P.add("act", (lambda nm=nm, s0=s0: lambda e: e.activation(out=s0.t[:, 3:4], in_=nm.t[:, 256:257], func=AF.Abs))(),
                  reads=[nm.b], writes=[s0.b])
            P.add("dve", (lambda nm=nm, s0=s0: lambda e: e.tensor_scalar(s0.t[:, 0:1], s0.t[:, 3:4], 1.0, None, ALU.max))(),
                  reads=[s0.b], writes=[s0.b])
            P.add("dve", (lambda s0=s0: lambda e: e.reciprocal(out=s0.t[:, 1:2], in_=s0.t[:, 0:1]))(), reads=[s0.b], writes=[s0.b])
            P.add("dve", (lambda nm=nm, s0=s0, h_=h_: lambda e: e.tensor_scalar(h_.t, nm.t[:, 0:256], s0.t[:, 1:2], None, ALU.mult))(),
                  reads=[nm.b, s0.b], writes=[h_.b])
            P.add("act", (lambda h_=h_, s1=s1: lambda e: e.activation(out=jk.t, in_=h_.t, func=AF.Square, accum_out=s1.t[:, 0:1]))(),
                  reads=[h_.b], writes=[jk.b, s1.b])
            P.add("act", (lambda s1=s1: lambda e: e.activation(out=s1.t[:, 1:2], in_=s1.t[:, 0:1], func=AF.Sqrt, scale=1.0 / 256,
                                                             bias=C.epsc.t[:, 0:1]))(),
                  reads=[s1.b, C.epsc.b], writes=[s1.b])
            P.add("dve", (lambda s1=s1: lambda e: e.reciprocal(out=s1.t[:, 2:3], in_=s1.t[:, 1:2]))(), reads=[s1.b], writes=[s1.b])
            P.add("dve", (lambda h_=h_, s1=s1, hd=hd: lambda e: e.scalar_tensor_tensor(
                out=h_.t, in0=h_.t, scalar=s1.t[:, 2:3], in1=whn.t[:, hd * 256:(hd + 1) * 256], op0=ALU.mult, op1=ALU.mult))(),
                reads=[h_.b, s1.b, whn.b], writes=[h_.b])
            P.add("dve", (lambda h_=h_, tt=tt, hd=hd: lambda e: e.tensor_tensor(
                out=hg3[:, tt, hd * 256:(hd + 1) * 256], in0=h_.t, in1=so3[:, tt, :], op=ALU.mult))(),
                reads=[h_.b, so.b], writes=[hg_parts[tt]])
            if tt < 7:
                state_update(kt3, vo3, vo.b, t16, hd, False, i)
    P.fence()
    ar.o = mark
    for tt in range(8):
        for q in range(4):
            pt = C.ps[6 + (q % 2)]
            ptb = pt.t.bitcast(BF16)
            for j in range(4):
                kc = q * 4 + j
                P.add("pe", (lambda kc=kc, j=j, tt=tt, ptb=ptb: lambda e: e.transpose(
                    out=ptb[:, j * 128:(j + 1) * 128], in_=hg3[:, tt, kc * 128:(kc + 1) * 128], identity=C.ident.t[:, :]))(),
                    reads=[hg_parts[tt], C.ident.b], writes=[pt.b])
            P.add("dve", (lambda q=q, tt=tt, ptb=ptb: lambda e: e.tensor_copy(
                out=xp3.t[:, q * 4:(q + 1) * 4, tt * 128:(tt + 1) * 128],
                in_=ptb[:, 0:512].rearrange("p (k t) -> p k t", k=4)))(),
                reads=[pt.b], writes=[xp.b])
    wo = []
    for i in range(2):
        w = ar.take(16 * 512, BF16, "mo_w%d" % i)
        w.t = w.t.rearrange("p (k c) -> p k c", k=16)
        wo.append(w)
    residual_out_proj(C, ar, xp3, w_out, wo, x_own, x_b, h_out, h_out_b)
    P.fence()


ARENA_EL = 94240


def _mk(nc):
    def din(n, s, dt=F32):
        return nc.dram_tensor(n, list(s), dt, kind="ExternalInput").ap()

    def dout(n, s, dt=F32):
        return nc.dram_tensor(n, list(s), dt, kind="ExternalOutput").ap()
    return din, dout


def build_mlstm():
    nc = bass.Bass("TRN2", target_bir_lowering=False)
    din, dout = _mk(nc)
    ident = din("ident", (128, 128))
    x_own = din("x_own", (1024, 2048)); x_prev = din("x_prev", (1024, 2048)); gbc = din("gbc_a", (128, 2048))
    w_in = din("w_in", (2048, 6160)); biasg = din("biasg", (128, 16)); whn = din("whn", (128, 2048))
    tri = din("tri", (128, 128)); negm = din("negm", (128, 128)); ones = din("ones", (128, 128))
    w_out = din("w_out", (2048, 2048))
    h_out = dout("h_out", (1024, 2048))
    P = Prog(nc)
    C = Ctx(P, nc, ident)
    ar = Arena(P, "arena", ARENA_EL)
    mlstm_phase(C, ar, x_own, x_prev, Buf("x"), gbc, w_in, biasg, whn, tri, negm, ones, w_out, h_out, Buf("ho"))
    P.build()
    return nc


def build_ffn(with_kv, final):
    nc = bass.Bass("TRN2", target_bir_lowering=False)
    din, dout = _mk(nc)
    ident = din("ident", (128, 128))
    h_in = din("h_in", (1024, 2048)); halo = din("halo", (128, 2048)); gbc = din("gbc_f", (128, 2048))
    w_up = din("w_up", (2048, 11264)); cw = din("cw", (128, 264)); cb = din("cb", (128, 88))
    w_down = din("w_down", (5632, 2048))
    fg = din("fg", (128, 2048)) if final else None
    h_out = dout("h_out", (1024, 2048))
    P = Prog(nc)
    C = Ctx(P, nc, ident)
    ar = Arena(P, "arena", ARENA_EL)
    A1 = V(ar.tile.t[:, 0:45056], "A1")
    A2 = V(ar.tile.t[:, 45056:45056 + 49184], "A2")
    hob = Buf("h_out")
    if with_kv:
        pos = din("pos", (128, 8), I32); invf = din("invf", (128, 16)); gkv = din("gbc_kv", (128, 2048))
        w_kv = din("w_kv", (2048, 4096))
        kT = dout("kT", (16, 128, 1024), BF16); v = dout("v", (1024, 2048), BF16)
        cos4, sin4 = rope_tables(C, pos, invf)
    ffn_phase(C, A1, A2, h_in, halo, gbc, w_up, cw, cb, w_down, h_out, final_gbc=fg, h_out_b=hob)
    if with_kv:
        kv_phase(C, ar, h_out, hob, gkv, w_kv, cos4, sin4, kT, v, Buf("kv"))
    P.build()
    return nc


def build_attn(lambda_init):
    nc = bass.Bass("TRN2", target_bir_lowering=False)
    din, dout = _mk(nc)
    ident = din("ident", (128, 128))
    h_in = din("h_in", (1024, 2048)); gbc = din("gbc_b", (128, 2048)); w_q = din("w_q", (2048, 2048))
    pos = din("pos", (128, 8), I32); invf = din("invf", (128, 16))
    kT_prev = din("kT_prev", (16, 128, 1024), BF16); kT_own = din("kT_own", (16, 128, 1024), BF16)
    v_prev = din("v_prev", (1024, 2048), BF16); v_own = din("v_own", (1024, 2048), BF16)
    lam = din("lam", (128, 512)); subln = din("subln", (128, 256)); tri = din("tri", (128, 128))
    flag = din("flag", (128, 1))
    w_o = din("w_o", (2048, 2048))
    h_out = dout("h_out", (1024, 2048))
    P = Prog(nc)
    C = Ctx(P, nc, ident)
    C.flag = P.tile("flag", [128, 1], F32)
    P.dma("sp", C.flag.t[:, :], flag, C.flag.b, writes=[C.flag.b])
    ar = Arena(P, "arena", ARENA_EL)
    cos4, sin4 = rope_tables(C, pos, invf)
    attn_phase(C, ar, h_in, Buf("hi"), gbc, w_q, cos4, sin4, kT_prev, kT_own, v_prev, v_own, Buf("kv"),
               lam, subln, tri, w_o, h_out, Buf("ho"), lambda_init)
    P.build()
    return nc


_PROGS = {}


def _prog(key, fn):
    if key not in _PROGS:
        _PROGS[key] = fn()
    return _PROGS[key]


def _rep(v, n=128):
    return np.ascontiguousarray(np.broadcast_to(np.asarray(v, np.float32).reshape(1, -1), (n, np.asarray(v).size)))


def _run(nc, maps):
    res = run_bass_kernel_spmd(nc, maps, core_ids=list(range(NCORES)))
    return res.results


DEBUG = {}


def kernel(x, positions, a_norm, m_w_in, m_b_igate, m_b_fgate, m_w_hnorm, m_w_out,
           kv_norm, w_kv, b_norm, w_q, lam_q1, lam_k1, lam_q2, lam_k2, subln, w_o,
           f_norm, w_up, conv_w, conv_b, w_down, final_norm):
    import math
    f32 = np.float32
    x = np.asarray(x, f32)
    positions = np.asarray(positions, np.int32)
    ident = np.eye(128, dtype=f32)
    tri = np.triu(np.ones((128, 128), f32))
    negm = np.where(tri > 0, 0.0, -30000.0).astype(f32)
    ones = np.ones((128, 128), f32)
    invf = _rep(np.float32(500000.0) ** (-(np.arange(16, dtype=np.float32)) / np.float32(16)))
    cores = [(b, hf) for b in range(4) for hf in range(2)]

    def own(a, b, hf):
        return np.ascontiguousarray(a[b, hf * 1024:(hf + 1) * 1024])

    nc1 = _prog("mlstm", build_mlstm)
    maps = []
    for (b, hf) in cores:
        xp = own(x, b, 0) if hf == 1 else np.zeros((1024, 2048), f32)
        maps.append(dict(ident=ident, x_own=own(x, b, hf), x_prev=xp, gbc_a=_rep(a_norm[0]),
                         w_in=np.asarray(m_w_in[0], f32),
                         biasg=_rep(np.concatenate([np.asarray(m_b_igate[0], f32), np.asarray(m_b_fgate[0], f32)])),
                         whn=_rep(np.asarray(m_w_hnorm[0], f32).reshape(-1)), tri=tri, negm=negm, ones=ones,
                         w_out=np.asarray(m_w_out[0], f32)))
    r1 = _run(nc1, maps)
    h0p = [r["h_out"] for r in r1]
    DEBUG["h0p"] = h0p

    def halo_of(hs, i):
        b, hf = cores[i]
        hl = np.zeros((128, 2048), f32)
        if hf == 1:
            hl[0:2] = hs[i - 1][1022:1024]
        return hl

    def cwl(l):
        cw = np.concatenate([np.asarray(conv_w[l][j], f32).reshape(88, 128).T for j in range(3)], axis=1)
        cb = np.asarray(conv_b[l], f32).reshape(88, 128).T
        return np.ascontiguousarray(cw), np.ascontiguousarray(cb)

    def posl(b, hf):
        return np.ascontiguousarray(positions[b, hf * 1024:(hf + 1) * 1024].reshape(8, 128).T.astype(np.int32))

    nc2 = _prog("ffnkv", lambda: build_ffn(True, False))
    cw0, cb0 = cwl(0)
    maps = []
    for i, (b, hf) in enumerate(cores):
        maps.append(dict(ident=ident, h_in=h0p[i], halo=halo_of(h0p, i), gbc_f=_rep(f_norm[0]),
                         w_up=np.asarray(w_up[0], f32), cw=cw0, cb=cb0, w_down=np.asarray(w_down[0], f32),
                         pos=posl(b, hf), invf=invf, gbc_kv=_rep(kv_norm), w_kv=np.asarray(w_kv, f32)))
    r2 = _run(nc2, maps)
    h1 = [r["h_out"] for r in r2]
    kTs = [r["kT"] for r in r2]
    vs = [r["v"] for r in r2]
    DEBUG.update(h1=h1, kT=kTs, v=vs)

    lambda_init = 0.8 - 0.6 * math.exp(-0.3 * 1)
    nc3 = _prog("attn", lambda: build_attn(lambda_init))
    lam = _rep(np.concatenate([np.asarray(a[0], f32) for a in (lam_q1, lam_k1, lam_q2, lam_k2)]))
    maps = []
    for i, (b, hf) in enumerate(cores):
        if hf == 1:
            kp, vp, fl = kTs[i - 1], vs[i - 1], np.ones((128, 1), f32)
        else:
            kp, vp, fl = np.zeros_like(kTs[i]), np.zeros_like(vs[i]), np.zeros((128, 1), f32)
        maps.append(dict(ident=ident, h_in=h1[i], gbc_b=_rep(b_norm[0]), w_q=np.asarray(w_q[0], f32), pos=posl(b, hf),
                         invf=invf, kT_prev=kp, kT_own=kTs[i], v_prev=vp, v_own=vs[i], lam=lam,
                         subln=_rep(subln[0]), tri=tri, flag=fl, w_o=np.asarray(w_o[0], f32)))
    r3 = _run(nc3, maps)
    h1p = [r["h_out"] for r in r3]
    DEBUG["h1p"] = h1p

    nc4 = _prog("ffnfinal", lambda: build_ffn(False, True))
    cw1, cb1 = cwl(1)
    maps = []
    for i, (b, hf) in enumerate(cores):
        maps.append(dict(ident=ident, h_in=h1p[i], halo=halo_of(h1p, i), gbc_f=_rep(f_norm[1]),
                         w_up=np.asarray(w_up[1], f32), cw=cw1, cb=cb1, w_down=np.asarray(w_down[1], f32),
                         fg=_rep(final_norm)))
    r4 = _run(nc4, maps)
    out = np.zeros((4, 2048, 2048), f32)
    for i, (b, hf) in enumerate(cores):
        out[b, hf * 1024:(hf + 1) * 1024] = r4[i]["h_out"]
    return out
```
